# Optimizing a Trainium2 kernel written in Bass

```python
import math
import jax, jax.numpy as jnp
from jax import lax
import numpy as np

D_MODEL = 2048
BATCH = 4
SEQ = 2048
DEPTH = 1

CHUNK = 64
PLE_DIM = 256
GM_WIDTH = D_MODEL
GM_GROUPS = 8
GM_GROUP_DIM = GM_WIDTH // GM_GROUPS
GM_BLOCK = 128
CV_WIDTH = D_MODEL
CV_KERNEL = 31
PEER_HEADS = 8
PEER_NKEYS = 128
PEER_EXPERTS = PEER_NKEYS * PEER_NKEYS
PEER_QDIM = 256
PEER_HALF = PEER_QDIM // 2
PEER_TOPK = 16
PEER_ACTIVE = PEER_HEADS * PEER_TOPK
PEER_TOKEN_BLOCK = 128
ALPHA = (2.0 * DEPTH) ** 0.25
BETA = (8.0 * DEPTH) ** -0.25
LN_EPS = 1e-5
IN_COLS = 2 * GM_WIDTH + 2 * CV_WIDTH + 2 * D_MODEL
IN_SPLITS = [GM_WIDTH, 2 * GM_WIDTH, 2 * GM_WIDTH + CV_WIDTH,
             2 * GM_WIDTH + 2 * CV_WIDTH, 2 * GM_WIDTH + 2 * CV_WIDTH + D_MODEL]

kernel_name = 'hybrid_gmlp_conformer_peer_block'


def layer_norm(x, g, b):
    xf = x.astype(jnp.float32)
    mu = jnp.mean(xf, axis=-1, keepdims=True)
    var = jnp.mean(jnp.square(xf - mu), axis=-1, keepdims=True)
    return ((xf - mu) * lax.rsqrt(var + LN_EPS) * g + b).astype(x.dtype)


def chunk_causal_mask(dtype):
    pos = jnp.arange(GM_BLOCK)
    return (pos[None, :] // CHUNK <= pos[:, None] // CHUNK).astype(dtype)


def spatial_gating(u, v, ln_g, ln_b, w_s, b_s):
    bsz, seq, _ = v.shape
    nblk = seq // GM_BLOCK
    v = v.reshape(bsz, seq, GM_GROUPS, GM_GROUP_DIM)
    v = layer_norm(v, ln_g.reshape(GM_GROUPS, GM_GROUP_DIM), ln_b.reshape(GM_GROUPS, GM_GROUP_DIM))
    v = v.reshape(bsz, nblk, GM_BLOCK, GM_GROUPS, GM_GROUP_DIM)
    w = w_s * chunk_causal_mask(w_s.dtype)
    mixed = jnp.einsum('gij,bnjgd->bnigd', w, v) + b_s.T[None, None, :, :, None]
    return u * mixed.reshape(bsz, seq, GM_WIDTH)


def causal_depthwise_conv(a, w, b):
    c = a.shape[-1]
    y = lax.conv_general_dilated(a, w[:, None, :].astype(a.dtype), window_strides=(1,),
                                 padding=[(CV_KERNEL - 1, 0)],
                                 dimension_numbers=('NWC', 'WIO', 'NWC'),
                                 feature_group_count=c)
    return y + b


def peer_route(xf, w_q, sub_keys):
    t = xf.shape[0]
    q = (xf @ w_q).reshape(t, PEER_HEADS, 2, PEER_HALF)
    s = jnp.einsum('thcd,hckd->thck', q, sub_keys).astype(jnp.float32)
    top_s, top_i = lax.top_k(s, PEER_TOPK)
    cand = (top_s[:, :, 0, :, None] + top_s[:, :, 1, None, :]).reshape(t, PEER_HEADS, PEER_TOPK * PEER_TOPK)
    best_s, best_c = lax.top_k(cand, PEER_TOPK)
    i1 = jnp.take_along_axis(top_i[:, :, 0], best_c // PEER_TOPK, axis=-1)
    i2 = jnp.take_along_axis(top_i[:, :, 1], best_c % PEER_TOPK, axis=-1)
    idx = i1 * PEER_NKEYS + i2
    gate = jax.nn.softmax(best_s, axis=-1)
    return idx.reshape(t, PEER_ACTIVE), gate.reshape(t, PEER_ACTIVE)


def peer_experts(xf, idx, gate, u_tab, v_tab):
    t, d = xf.shape
    nblk = t // PEER_TOKEN_BLOCK

    def block(args):
        xb, ib, gb = args
        u = jnp.take(u_tab, ib, axis=0)
        h = jnp.einsum('tkd,td->tk', u, xb).astype(jnp.float32)
        act = (gb * jax.nn.gelu(h, approximate=False)).astype(xb.dtype)
        v = jnp.take(v_tab, ib, axis=0)
        return jnp.einsum('tk,tkd->td', act, v)

    y = lax.map(block, (xf.reshape(nblk, PEER_TOKEN_BLOCK, d),
                        idx.reshape(nblk, PEER_TOKEN_BLOCK, PEER_ACTIVE),
                        gate.reshape(nblk, PEER_TOKEN_BLOCK, PEER_ACTIVE)))
    return y.reshape(t, d)


def setup_inputs(seed: int = 0) -> dict:
    key = jax.random.key(seed)
    ks = jax.random.split(key, 32)
    f32 = jnp.float32
    n = lambda k, shape: jax.random.normal(k, shape, f32)
    L = DEPTH
    return {
        'x': n(ks[0], (BATCH, SEQ, D_MODEL)),
        'p': n(ks[1], (DEPTH, BATCH, SEQ, PLE_DIM)),
        'w_in': n(ks[2], (L, D_MODEL, IN_COLS)) * D_MODEL ** -0.5,
        'b_in': n(ks[3], (L, IN_COLS)) * 0.01,
        'gm_ln_g': 1.0 + 0.01 * n(ks[4], (L, GM_WIDTH)),
        'gm_ln_b': 0.01 * n(ks[5], (L, GM_WIDTH)),
        'gm_ws': n(ks[6], (L, GM_GROUPS, GM_BLOCK, GM_BLOCK)) * GM_BLOCK ** -0.5,
        'gm_bs': 1.0 + 0.01 * n(ks[7], (L, GM_GROUPS, GM_BLOCK)),
        'w_gm_out': n(ks[8], (L, GM_WIDTH, D_MODEL)) * GM_WIDTH ** -0.5 * BETA,
        'cv_w': n(ks[9], (L, CV_KERNEL, CV_WIDTH)) * CV_KERNEL ** -0.5,
        'cv_b': 0.01 * n(ks[10], (L, CV_WIDTH)),
        'cv_ln_g': 1.0 + 0.01 * n(ks[11], (L, CV_WIDTH)),
        'cv_ln_b': 0.01 * n(ks[12], (L, CV_WIDTH)),
        'w_cv_out': n(ks[13], (L, CV_WIDTH, D_MODEL)) * CV_WIDTH ** -0.5 * BETA,
        'w_o': n(ks[14], (L, D_MODEL, D_MODEL)) * D_MODEL ** -0.5 * BETA,
        'ln1_g': 1.0 + 0.01 * n(ks[15], (L, D_MODEL)),
        'ln1_b': 0.01 * n(ks[16], (L, D_MODEL)),
        'peer_wq': n(ks[17], (L, D_MODEL, PEER_HEADS * PEER_QDIM)) * D_MODEL ** -0.5,
        'peer_keys': n(ks[18], (L, PEER_HEADS, 2, PEER_NKEYS, PEER_HALF)) * PEER_HALF ** -0.5,
        'peer_u': n(ks[19], (L, PEER_EXPERTS, D_MODEL)) * D_MODEL ** -0.5,
        'peer_v': n(ks[20], (L, PEER_EXPERTS, D_MODEL)) * PEER_ACTIVE ** -0.5 * BETA,
        'ln2_g': 1.0 + 0.01 * n(ks[21], (L, D_MODEL)),
        'ln2_b': 0.01 * n(ks[22], (L, D_MODEL)),
        'ple_w_gate': n(ks[23], (L, D_MODEL, D_MODEL)) * D_MODEL ** -0.5,
        'ple_w_proj': n(ks[24], (L, PLE_DIM, D_MODEL)) * PLE_DIM ** -0.5 * BETA,
    }


def reference(x, p, w_in, b_in, gm_ln_g, gm_ln_b, gm_ws, gm_bs, w_gm_out, cv_w, cv_b,
              cv_ln_g, cv_ln_b, w_cv_out, w_o, ln1_g, ln1_b, peer_wq, peer_keys,
              peer_u, peer_v, ln2_g, ln2_b, ple_w_gate, ple_w_proj):
    bsz, seq, d = x.shape
    for i in range(DEPTH):
        z = x @ w_in[i] + b_in[i]
        zu, zv, za, zb, zga, zgb = jnp.split(z, IN_SPLITS, axis=-1)
        sg = spatial_gating(jax.nn.gelu(zu, approximate=False), jax.nn.gelu(zv, approximate=False),
                            gm_ln_g[i], gm_ln_b[i], gm_ws[i], gm_bs[i])
        y_a = sg @ w_gm_out[i]
        c = causal_depthwise_conv(za * jax.nn.sigmoid(zb), cv_w[i], cv_b[i])
        c = jax.nn.silu(layer_norm(c, cv_ln_g[i], cv_ln_b[i]))
        y_b = c @ w_cv_out[i]
        mix = (jax.nn.sigmoid(zga) * y_a + jax.nn.sigmoid(zgb) * y_b) @ w_o[i]
        x = layer_norm(ALPHA * x + mix, ln1_g[i], ln1_b[i])
        xf = x.reshape(bsz * seq, d)
        idx, gate = peer_route(xf, peer_wq[i], peer_keys[i])
        y_ff = peer_experts(xf, idx, gate, peer_u[i], peer_v[i]).reshape(bsz, seq, d)
        x = layer_norm(ALPHA * x + y_ff, ln2_g[i], ln2_b[i])
        x = x + jax.nn.sigmoid(x @ ple_w_gate[i]) * (p[i] @ ple_w_proj[i])
    return x
```

```python
import numpy as np
import concourse.bass as bass
import concourse.mybir as mybir
from concourse.bass_utils import run_bass_kernel_spmd

F32 = mybir.dt.float32
BF16 = mybir.dt.bfloat16
I32 = mybir.dt.int32
AF = mybir.ActivationFunctionType
ALU = mybir.AluOpType
AX = mybir.AxisListType

CENGS = ['pe', 'act', 'dve', 'pool']
ENGS = CENGS + ['sp']
KDMA = 16
ALPHA = 2.0 ** 0.25
EPS = 1e-5
NEG = -1.0e30


class T:
    __slots__ = ('name', 'w', 'r', 'strict')

    def __init__(self, name='', strict=False):
        self.name = name
        self.w = None
        self.r = []
        self.strict = strict


class Sched:
    def __init__(self, nc):
        self.nc = nc
        self.ops = {e: [] for e in ENGS}
        self.known = {e: self._empty() for e in ENGS}
        self.sp_gate = None

    @staticmethod
    def _empty():
        k = {f: -1 for f in CENGS}
        k['dma'] = [-1] * KDMA
        return k

    @staticmethod
    def _merge(a, b):
        for f in CENGS:
            if b[f] > a[f]:
                a[f] = b[f]
        da, db = a['dma'], b['dma']
        for i in range(KDMA):
            if db[i] > da[i]:
                da[i] = db[i]

    def _is_known(self, e, dep):
        f, i = dep
        k = self.known[e]
        if f == 'sp':
            return k['dma'][i % KDMA] >= i
        return k[f] >= i

    def add(self, eng, fn, reads=(), writes=(), extra=()):
        ops = self.ops[eng]
        idx = len(ops)
        deps = list(extra)
        if eng == 'sp' and self.sp_gate:
            deps += self.sp_gate
            self.sp_gate = None
        for t in reads:
            if t.w is not None:
                deps.append(t.w)
        for t in writes:
            if t.w is not None:
                deps.append(t.w)
            deps.extend(t.r)
        k = self.known[eng]
        waits = []
        if eng == 'sp' and idx >= KDMA:
            deps.append(('sp', idx - KDMA))
        deps = sorted(set(deps), key=lambda d: -d[1])
        for d in deps:
            f, i = d
            if f == eng and f != 'sp':
                if f == 'pe':
                    continue
                if i >= idx:
                    continue
            if self._is_known(eng, d):
                continue
            waits.append(d)
            self._merge(k, self.ops[f][i]['snap'])
            if f == 'sp':
                k['dma'][i % KDMA] = max(k['dma'][i % KDMA], i)
            else:
                k[f] = max(k[f], i)
        snap = {f: k[f] for f in CENGS}
        snap['dma'] = list(k['dma'])
        ops.append(dict(fn=fn, waits=waits, snap=snap, sig=False))
        for d in waits:
            self.ops[d[0]][d[1]]['sig'] = True
        me = (eng, idx)
        for t in writes:
            if t.strict and t.w is not None and not t.r and t not in reads:
                raise RuntimeError(f"tile {t.name}: overwritten before being read")
        for t in reads:
            t.r.append(me)
        for t in writes:
            t.w = me
            t.r = []
        return me

    def barrier(self):
        deps = [(e, len(self.ops[e]) - 1) for e in CENGS if self.ops[e]]
        n = len(self.ops['sp'])
        deps += [('sp', j) for j in range(max(0, n - KDMA), n)]
        for e in CENGS:
            self.add(e, lambda eng: eng.nop(), extra=deps)
        self.sp_gate = deps

    def emit(self, sems, dsems):
        nc = self.nc
        rank = {}
        for e in CENGS:
            c = 0
            for i, op in enumerate(self.ops[e]):
                if op['sig']:
                    c += 1
                    rank[(e, i)] = c
        ndma = len(self.ops['sp'])

        def run(e, eng):
            for i, op in enumerate(self.ops[e]):
                for (f, j) in op['waits']:
                    if f == 'sp':
                        eng.wait_ge(dsems[j % KDMA], 16 * (j // KDMA + 1))
                    else:
                        eng.wait_ge(sems[f], rank[(f, j)])
                ins = op['fn'](eng)
                if e == 'sp':
                    ins.then_inc(dsems[i % KDMA], 16)
                elif op['sig']:
                    ins.then_inc(sems[e], 1)
            if e == 'sp':
                for s in range(KDMA):
                    cnt = len(range(s, ndma, KDMA))
                    if cnt:
                        eng.wait_ge(dsems[s], 16 * cnt)

        with nc.Block() as block:
            @block.sync
            def _(eng):
                run('sp', eng)

            @block.tensor
            def _(eng):
                run('pe', eng)

            @block.vector
            def _(eng):
                run('dve', eng)

            @block.scalar
            def _(eng):
                run('act', eng)

            @block.gpsimd
            def _(eng):
                run('pool', eng)


class Arena:
    def __init__(self, nc, nbytes):
        self.n = nbytes // 2
        self.t = nc.alloc_sbuf_tensor("arena", [128, self.n], BF16)
        self.free = [(0, nbytes)]

    def alloc(self, nbytes):
        nbytes = (nbytes + 63) // 64 * 64
        for k, (o, s) in enumerate(self.free):
            if s >= nbytes:
                if s == nbytes:
                    self.free.pop(k)
                else:
                    self.free[k] = (o + nbytes, s - nbytes)
                return (o, nbytes)
        raise RuntimeError(f"arena full: need {nbytes}, free {self.free}")

    def release(self, blk):
        self.free.append(blk)
        self.free.sort()
        m = []
        for o, s in self.free:
            if m and m[-1][0] + m[-1][1] == o:
                m[-1] = (m[-1][0], m[-1][1] + s)
            else:
                m.append((o, s))
        self.free = m

    def view(self, blk, shape, dt):
        o, s = blk
        esz = 4 if dt in (F32, I32) else 2
        nel = 1
        for d in shape[1:]:
            nel *= d
        assert nel * esz <= s, (shape, s)
        ap = self.t[:, o // 2: o // 2 + nel * esz // 2]
        if esz == 4:
            ap = ap.bitcast(dt)
        if len(shape) == 3:
            ap = ap.rearrange("p (a b) -> p a b", a=shape[1])
        elif len(shape) == 4:
            ap = ap.rearrange("p (a b c) -> p a b c", a=shape[1], b=shape[2])
        return ap


CST_COLS = 96 + 16 * 4 + 1 + 16 * 31
C_B, C_GMG, C_CVB, C_CVG, C_CVBB, C_FLAG, C_CVW = 0, 96, 112, 128, 144, 160, 161


def build(NTB=8, NCH=128, dbg=None, stop_after=None):
    TOK = 128 * NTB
    NW = min(512, TOK)
    NH = TOK // NW
    TBH = NW // 128
    HALO = 32
    XW = TOK + HALO
    nc = bass.Bass("TRN2", target_bir_lowering=False)

    def din(name, shape, dt=F32):
        return nc.dram_tensor(name, list(shape), dt, kind="ExternalInput").ap()

    xT = din("xT", [2048, XW])
    xtm = din("x", [TOK, 2048])
    pT = din("pT", [256, TOK])
    cst_d = din("cst", [128, CST_COLS])
    w_in = din("w_in", [2048, 12288])
    b_v_bc = din("b_v_bc", [128, 2048])
    gm_b_bc = din("gm_b_bc", [128, 2048])
    gm_wsT = din("gm_wsT", [128, 8, 128])
    gm_bs = din("gm_bs", [1, 1024])
    w_a = din("w_a", [2048, 2048])
    w_b = din("w_b", [2048, 2048])
    w_o = din("w_o", [2048, 2048])
    ln1_g_bc = din("ln1_g_bc", [128, 2048])
    ln1_b_bc = din("ln1_b_bc", [128, 2048])
    w_q = din("w_q", [2048, 2048])
    keysT = din("keysT", [128, 16, 128])
    uT = din("uT", [2048, 128, 128])
    vP = din("vP", [128, 128, 2048])
    ln2_g_bc = din("ln2_g_bc", [128, 2048])
    ln2_b_bc = din("ln2_b_bc", [128, 2048])
    w_pg = din("w_pg", [2048, 2048])
    w_pe = din("w_pe", [256, 2048])
    out = nc.dram_tensor("out", [TOK, 2048], F32, kind="ExternalOutput").ap()
    g_scr = nc.dram_tensor("g_scr", [128, 128, TOK], BF16, kind="Internal").ap()
    x1_scr = nc.dram_tensor("x1_scr", [TOK, 2048], F32, kind="Internal").ap()
    s_scr = nc.dram_tensor("s_scr", [TOK, 2048], F32, kind="Internal").ap()
    dbg_out = {}
    if dbg:
        for name, shape in dbg.items():
            dbg_out[name] = nc.dram_tensor("dbg_" + name, list(shape), F32, kind="ExternalOutput").ap()

    S = Sched(nc)
    AR = Arena(nc, 198 * 1024)
    PS = [nc.alloc_psum_tensor(f"ps{i}", [128, 512], F32)[:, :] for i in range(8)]
    TP = [T(f"ps{i}") for i in range(8)]
    pcnt = [0]

    def nb():
        i = pcnt[0] % 8
        pcnt[0] += 1
        return i

    def dma(out_, in_, reads=(), writes=()):
        return S.add('sp', lambda e: e.dma_start(out=out_, in_=in_), reads, writes)

    def mm(out_, lhsT, rhs, start, stop, reads, writes):
        return S.add('pe', lambda e: e.matmul(out_, lhsT=lhsT, rhs=rhs, start=start, stop=stop), reads, writes)

    def tr(out_, in_, ident, reads, writes):
        return S.add('pe', lambda e: e.transpose(out=out_, in_=in_, identity=ident), reads, writes)

    def act(out_, in_, func, reads, writes, bias=None, scale=None):
        kw = {}
        if bias is not None:
            kw['bias'] = bias
        if scale is not None:
            kw['scale'] = scale
        return S.add('act', lambda e: e.activation(out=out_, in_=in_, func=func, **kw), reads, writes)

    def tsc(eng, out_, in0, s1, s2, op0, op1, reads, writes):
        if op1 is None:
            return S.add(eng, lambda e: e.tensor_scalar(out=out_, in0=in0, scalar1=s1, scalar2=None, op0=op0), reads, writes)
        return S.add(eng, lambda e: e.tensor_scalar(out=out_, in0=in0, scalar1=s1, scalar2=s2, op0=op0, op1=op1), reads, writes)

    def stt(out_, in0, sc, in1, op0, op1, reads, writes):
        return S.add('dve', lambda e: e.scalar_tensor_tensor(out=out_, in0=in0, scalar=sc, in1=in1, op0=op0, op1=op1), reads, writes)

    def tt(eng, out_, in0, in1, op, reads, writes):
        return S.add(eng, lambda e: e.tensor_tensor(out=out_, in0=in0, in1=in1, op=op), reads, writes)

    def cp(eng, out_, in_, reads, writes):
        if eng == 'act':
            return S.add('act', lambda e: e.activation(out=out_, in_=in_, func=AF.Copy), reads, writes)
        return S.add(eng, lambda e: e.tensor_copy(out=out_, in_=in_), reads, writes)

    def gen(eng, fn, reads, writes):
        return S.add(eng, fn, reads, writes)

    b_cst = AR.alloc(CST_COLS * 4)
    cst = AR.view(b_cst, [128, CST_COLS], F32)
    t_cst = T('cst')
    dma(cst, cst_d, writes=[t_cst])
    b_id = AR.alloc(128 * 4 * 3 + 128 * 2)
    o0 = b_id[0]
    ident_f = AR.view((o0, 512), [128, 128], F32)
    ones_f = AR.view((o0 + 512, 512), [128, 128], F32)
    rep8 = AR.view((o0 + 1024, 512), [128, 128], F32)
    ident_b = AR.view((o0 + 1536, 256), [128, 128], BF16)
    t_id = T('ident')
    b_tmp = AR.alloc(512)
    iot = AR.view(b_tmp, [128, 128], I32)
    gen('pool', lambda e: e.iota(iot, pattern=[[1, 128]], base=0, channel_multiplier=-1), [], [t_id])
    tsc('dve', ident_f, iot, 0.0, None, ALU.is_equal, None, [t_id], [t_id])
    cp('dve', ident_b, ident_f, [t_id], [t_id])
    gen('dve', lambda e: e.memset(ones_f, 1.0), [], [t_id])
    gen('pool', lambda e: e.iota(iot, pattern=[[1, 8], [0, 16]], base=0, channel_multiplier=-1), [t_id], [t_id])
    tsc('dve', rep8, iot, 0.0, None, ALU.is_equal, None, [t_id], [t_id])
    S.barrier()
    AR.release(b_tmp)

    NST, NLT = 2, 4
    b_wst = [AR.alloc(16 * 128 * 4) for _ in range(NST)]
    b_wlt = [AR.alloc(16 * 128 * 2) for _ in range(NLT)]
    wst = [AR.view(b, [128, 16, 128], F32) for b in b_wst]
    wlt = [AR.view(b, [128, 16, 128], BF16) for b in b_wlt]
    t_wst = [T(f'wst{i}', True) for i in range(NST)]
    t_wlt = [T(f'wlt{i}', True) for i in range(NLT)]
    wcnt = [0, 0]
    ccnt = [0]

    def cast_eng():
        ccnt[0] += 1
        return 'act' if ccnt[0] % 2 else 'dve'

    def load_lhs(wd, col0, ceng=None):
        wv = wd.rearrange("(k p) c -> p k c", p=128)
        i = wcnt[0] % NST
        wcnt[0] += 1
        dma(wst[i], wv[:, :, col0:col0 + 128], writes=[t_wst[i]])
        j = wcnt[1] % NLT
        wcnt[1] += 1
        cp(ceng or cast_eng(), wlt[j], wst[i], [t_wst[i]], [t_wlt[j]])
        return wlt[j], t_wlt[j]

    def load_rhs(wd, col0, dst, t_dst, ceng=None):
        wv = wd.rearrange("(k p) c -> p k c", p=128)
        for q in range(4):
            i = wcnt[0] % NST
            wcnt[0] += 1
            dma(wst[i], wv[:, :, col0 + q * 128: col0 + (q + 1) * 128], writes=[t_wst[i]])
            cp(ceng or cast_eng(), dst[:, :, q * 128:(q + 1) * 128], wst[i], [t_wst[i]], [t_dst])

    def dump(name, src_ap, t_src, dst_ap=None):
        if name in dbg_out:
            dma(dbg_out[name] if dst_ap is None else dst_ap, src_ap, reads=[t_src])

    b_xTb = AR.alloc(16 * XW * 2)
    xTb = AR.view(b_xTb, [128, 16, XW], BF16)
    t_xTb = T('xTb')
    b_xst = [AR.alloc(XW * 4) for _ in range(2)]
    xst = [AR.view(b, [128, XW], F32) for b in b_xst]
    t_xst = [T(), T()]
    xTv = xT.rearrange("(k p) n -> p k n", p=128)
    for k in range(16):
        dma(xst[k % 2], xTv[:, k, :], writes=[t_xst[k % 2]])
        cp('act' if k % 2 else 'dve', xTb[:, k, :], xst[k % 2], [t_xst[k % 2]], [t_xTb])

    def col(c, n=1):
        return cst[:, c:c + n]

    halves = [(HALO + h * NW, NW) for h in range(NH)]

    def proj_fm(wl, t_wl, banks, with_halo=None):
        for h, (c0, n) in enumerate(halves):
            for k in range(16):
                mm(PS[banks[h]][:, 0:n], wl[:, k, :], xTb[:, k, c0:c0 + n], k == 0, k == 15, [t_wl, t_xTb], [TP[banks[h]]])
        if with_halo is not None:
            bk, off = with_halo
            for k in range(16):
                mm(PS[bk][:, off:off + HALO], wl[:, k, :], xTb[:, k, 0:HALO], k == 0, k == 15, [t_wl, t_xTb], [TP[bk]])

    b_cT = [AR.alloc(TOK * 4) for _ in range(16)]
    cTl = [AR.view(b, [128, TOK], F32) for b in b_cT]
    t_cT = [T() for _ in range(16)]
    b_glu = [AR.alloc(XW * 4) for _ in range(2)]
    glu = [AR.view(b, [128, XW], F32) for b in b_glu]
    t_glu = [T(), T()]
    sig = xst
    t_sig = t_xst

    def stageB_load(cc):
        la = load_lhs(w_in, 4096 + cc * 128)
        lb = load_lhs(w_in, 6144 + cc * 128)
        return la, lb

    def stageB_compute(cc, ld):
        (wa, t_wa), (wb, t_wb) = ld
        i = cc % 2
        ba = [nb() for _ in range(NH)]
        bh = nb()
        bb = [nb() for _ in range(NH)]
        proj_fm(wb, t_wb, bb, with_halo=(bh, 64))
        proj_fm(wa, t_wa, ba, with_halo=(bh, 0))
        act(sig[i][:, 0:HALO], PS[bh][:, 64:64 + HALO], AF.Sigmoid, [TP[bh], t_cst], [t_sig[i]], bias=col(C_B + 48 + cc))
        for h, (c0, n) in enumerate(halves):
            act(sig[i][:, c0:c0 + n], PS[bb[h]][:, 0:n], AF.Sigmoid, [TP[bb[h]], t_cst], [t_sig[i]], bias=col(C_B + 48 + cc))
        stt(glu[i][:, 0:HALO], PS[bh][:, 0:HALO], col(C_B + 32 + cc), sig[i][:, 0:HALO], ALU.add, ALU.mult,
            [TP[bh], t_sig[i], t_cst], [t_glu[i]])
        tsc('dve', glu[i][:, 0:HALO], glu[i][:, 0:HALO], col(C_FLAG), None, ALU.mult, None, [t_glu[i]], [t_glu[i]])
        for h, (c0, n) in enumerate(halves):
            stt(glu[i][:, c0:c0 + n], PS[ba[h]][:, 0:n], col(C_B + 32 + cc), sig[i][:, c0:c0 + n], ALU.add, ALU.mult,
                [TP[ba[h]], t_sig[i], t_cst], [t_glu[i]])
        acc = cTl[cc]
        tsc('dve', acc, glu[i][:, 2:2 + TOK], col(C_CVW + cc * 31), col(C_CVB + cc), ALU.mult, ALU.add,
            [t_glu[i], t_cst], [t_cT[cc]])
        for k in range(1, 31):
            stt(acc, glu[i][:, k + 2:k + 2 + TOK], col(C_CVW + cc * 31 + k), acc, ALU.mult, ALU.add,
                [t_glu[i], t_cst, t_cT[cc]], [t_cT[cc]])

    ld = stageB_load(0)
    for cc in range(16):
        nxt = stageB_load(cc + 1) if cc + 1 < 16 else None
        stageB_compute(cc, ld)
        ld = nxt
    if 'cT' in dbg_out:
        for cc in range(16):
            dump('cT', cTl[cc], t_cT[cc], dbg_out['cT'][cc * 128:(cc + 1) * 128, :])

    b_mean = AR.alloc(TOK * 4)
    b_rstd = AR.alloc(TOK * 4)
    mean = AR.view(b_mean, [128, TOK], F32)
    rstd = AR.view(b_rstd, [128, TOK], F32)
    t_mean, t_rstd = T(), T()
    sq = glu
    t_sq = t_glu
    for h in range(NH):
        c0 = h * NW
        b1, b2 = nb(), nb()
        for cc in range(16):
            mm(PS[b1][:, 0:NW], ones_f, cTl[cc][:, c0:c0 + NW], cc == 0, cc == 15, [t_id, t_cT[cc]], [TP[b1]])
        for cc in range(16):
            i = cc % 2
            act(sq[i][:, 0:NW], cTl[cc][:, c0:c0 + NW], AF.Square, [t_cT[cc]], [t_sq[i]])
            mm(PS[b2][:, 0:NW], ones_f, sq[i][:, 0:NW], cc == 0, cc == 15, [t_id, t_sq[i]], [TP[b2]])
        m_ = mean[:, c0:c0 + NW]
        r_ = rstd[:, c0:c0 + NW]
        tsc('dve', m_, PS[b1][:, 0:NW], 1.0 / 2048, None, ALU.mult, None, [TP[b1]], [t_mean])
        tt('dve', r_, m_, m_, ALU.mult, [t_mean], [t_rstd])
        stt(r_, PS[b2][:, 0:NW], 1.0 / 2048, r_, ALU.mult, ALU.subtract, [TP[b2], t_rstd], [t_rstd])
        tsc('dve', r_, r_, EPS, None, ALU.add, None, [t_rstd], [t_rstd])
        act(r_, r_, AF.Sqrt, [t_rstd], [t_rstd])
        gen('dve', lambda e, r_=r_: e.reciprocal(out=r_, in_=r_), [t_rstd], [t_rstd])
    b_cs = AR.alloc(16 * TOK * 2)
    csT = AR.view(b_cs, [128, 16, TOK], BF16)
    t_cs = T('csT')
    for cc in range(16):
        i = cc % 2
        tt('dve', sq[i][:, 0:TOK], cTl[cc], mean, ALU.subtract, [t_cT[cc], t_mean], [t_sq[i]])
        tt('dve', sq[i][:, 0:TOK], sq[i][:, 0:TOK], rstd, ALU.mult, [t_sq[i], t_rstd], [t_sq[i]])
        act(csT[:, cc, :], sq[i][:, 0:TOK], AF.Silu, [t_sq[i], t_cst], [t_cs], bias=col(C_CVBB + cc), scale=col(C_CVG + cc))
    if 'csT' in dbg_out:
        S.barrier()
        for cc in range(16):
            cp('dve', sq[0][:, 0:TOK], csT[:, cc, :], [t_cs], [t_sq[0]])
            dump('csT', sq[0][:, 0:TOK], t_sq[0], dbg_out['csT'][cc * 128:(cc + 1) * 128, :])
    S.barrier()
    for b in b_cT:
        AR.release(b)
    AR.release(b_mean)
    AR.release(b_rstd)
    for b in b_glu:
        AR.release(b)

    b_m = AR.alloc(16 * TOK * 2)
    mT = AR.view(b_m, [128, 16, TOK], BF16)
    t_m = [T() for _ in range(16)]

    def act_matmul_fm(wl, t_wl, src, t_src, banks):
        for h in range(NH):
            c0 = h * NW
            for k in range(16):
                mm(PS[banks[h]][:, 0:NW], wl[:, k, :], src[:, k, c0:c0 + NW], k == 0, k == 15, [t_wl, t_src], [TP[banks[h]]])

    def gated_branch(w_dram, gate_col0, bias_c0, src, t_src, add_prev):
        def load(oc):
            return load_lhs(w_dram, oc * 128), load_lhs(w_in, gate_col0 + oc * 128)

        ld = load(0)
        for oc in range(16):
            nxt = load(oc + 1) if oc + 1 < 16 else None
            (wy, t_wy), (wg, t_wg) = ld
            i = oc % 2
            by = [nb() for _ in range(NH)]
            bg = [nb() for _ in range(NH)]
            proj_fm(wg, t_wg, bg)
            act_matmul_fm(wy, t_wy, src, t_src, by)
            for h in range(NH):
                c0 = h * NW
                act(sig[i][:, c0:c0 + NW], PS[bg[h]][:, 0:NW], AF.Sigmoid, [TP[bg[h]], t_cst], [t_sig[i]], bias=col(bias_c0 + oc))
                if not add_prev:
                    tt('dve', mT[:, oc, c0:c0 + NW], PS[by[h]][:, 0:NW], sig[i][:, c0:c0 + NW], ALU.mult,
                       [TP[by[h]], t_sig[i]], [t_m[oc]])
                else:
                    tt('dve', sig[i][:, c0:c0 + NW], PS[by[h]][:, 0:NW], sig[i][:, c0:c0 + NW], ALU.mult,
                       [TP[by[h]], t_sig[i]], [t_sig[i]])
                    tt('dve', mT[:, oc, c0:c0 + NW], sig[i][:, c0:c0 + NW], mT[:, oc, c0:c0 + NW], ALU.add,
                       [t_sig[i], t_m[oc]], [t_m[oc]])
            ld = nxt

    gated_branch(w_b, 10240, C_B + 80, csT, t_cs, add_prev=False)
    if 'mB' in dbg_out:
        S.barrier()
        for fc in range(16):
            cp('dve', xst[0][:, 0:TOK], mT[:, fc, :], [t_m[fc]], [t_xst[0]])
            dump('mB', xst[0][:, 0:TOK], t_xst[0], dbg_out['mB'][fc * 128:(fc + 1) * 128, :])
    S.barrier()
    AR.release(b_cs)

    b_bc = [AR.alloc(2048 * 4) for _ in range(2)]
    bc = [AR.view(b, [128, 2048], F32) for b in b_bc]
    t_bc = [T(), T()]
    dma(bc[0], b_v_bc, writes=[t_bc[0]])
    dma(bc[1], gm_b_bc, writes=[t_bc[1]])
    b_ws = AR.alloc(8 * 128 * 4)
    b_wsb = AR.alloc(8 * 128 * 2)
    b_bs = AR.alloc(1024 * 4)
    b_CT = AR.alloc(16 * 128 * 4)
    wsf = AR.view(b_ws, [128, 8, 128], F32)
    wsb = AR.view(b_wsb, [128, 8, 128], BF16)
    bsr = AR.view(b_bs, [128, 1024], F32)
    CT = AR.view(b_CT, [128, 16, 128], F32)
    t_ws, t_bs, t_CT = T(), T(), T()
    dma(wsf, gm_wsT, writes=[t_ws])
    dma(bsr[0:1, :], gm_bs, writes=[t_bs])
    gen('dve', lambda e: e.memset(wsf[64:128, :, 0:64], 0.0), [t_ws], [t_ws])
    cp('dve', wsb, wsf, [t_ws], [t_ws])
    for fc in range(16):
        g = fc // 2
        bk = nb()
        mm(PS[bk][:, 0:128], bc[1][:, fc * 128:(fc + 1) * 128], wsf[:, g, :], True, False, [t_bc[1], t_ws], [TP[bk]])
        mm(PS[bk][:, 0:128], ones_f[0:1, :], bsr[0:1, g * 128:(g + 1) * 128], False, True, [t_id, t_bs], [TP[bk]])
        cp('act', CT[:, fc, :], PS[bk][:, 0:128], [TP[bk]], [t_CT])
    if 'CT' in dbg_out:
        dump('CT', CT.rearrange("p a b -> p (a b)"), t_CT)
    S.barrier()
    AR.release(b_bc[1])
    AR.release(b_ws)
    AR.release(b_bs)

    b_sg = AR.alloc(16 * TOK * 2)
    sgT = AR.view(b_sg, [128, 16, TOK], BF16)
    t_sg = T('sgT')
    b_wr = AR.alloc(16 * 512 * 2)
    wrh = AR.view(b_wr, [128, 16, 512], BF16)
    t_wr = T('wrh')
    b_vn = AR.alloc(NTB * 512 * 2)
    vn = AR.view(b_vn, [128, NTB, 512], BF16)
    t_vn = T('vn')
    b_t1 = [AR.alloc(512 * 4) for _ in range(2)]
    t1 = [AR.view(b, [128, 512], F32) for b in b_t1]
    t_t1 = [T(), T()]
    b_stA = AR.alloc(64 * 4)
    sttA = AR.view(b_stA, [128, 64], F32)
    t_stt = T()
    gu = [x_[:, 0:TOK] for x_ in xst]
    t_gu = t_xst

    for cg in range(4):
        load_rhs(w_in, 2048 + cg * 512, wrh, t_wr)
        for tb in range(NTB):
            i = tb % 2
            bk = nb()
            c0 = HALO + tb * 128
            for k in range(16):
                mm(PS[bk], xTb[:, k, c0:c0 + 128], wrh[:, k, :], k == 0, k == 15, [t_xTb, t_wr], [TP[bk]])
            tt('dve', t1[i], PS[bk], bc[0][:, cg * 512:(cg + 1) * 512], ALU.add, [TP[bk], t_bc[0]], [t_t1[i]])
            act(t1[i], t1[i], AF.Gelu, [t_t1[i]], [t_t1[i]])
            for g2 in range(2):
                seg = t1[i][:, g2 * 256:(g2 + 1) * 256]
                gen('dve', lambda e, seg=seg, g2=g2: e.bn_stats(out=sttA[:, g2 * 8:g2 * 8 + 6], in_=seg), [t_t1[i]], [t_stt])
                gen('dve', lambda e, g2=g2: e.bn_aggr(out=sttA[:, 16 + g2 * 2:18 + g2 * 2], in_=sttA[:, g2 * 8:g2 * 8 + 6]), [t_stt], [t_stt])
                tsc('dve', sttA[:, 24 + g2:25 + g2], sttA[:, 17 + g2 * 2:18 + g2 * 2], EPS, None, ALU.add, None, [t_stt], [t_stt])
                act(sttA[:, 24 + g2:25 + g2], sttA[:, 24 + g2:25 + g2], AF.Sqrt, [t_stt], [t_stt])
                gen('dve', lambda e, g2=g2: e.reciprocal(out=sttA[:, 24 + g2:25 + g2], in_=sttA[:, 24 + g2:25 + g2]), [t_stt], [t_stt])
                tsc('dve', vn[:, tb, g2 * 256:(g2 + 1) * 256], seg, sttA[:, 16 + g2 * 2:17 + g2 * 2], sttA[:, 24 + g2:25 + g2],
                    ALU.subtract, ALU.mult, [t_t1[i], t_stt], [t_vn])
            if 'gv' in dbg_out and cg == 0 and tb == 0:
                dump('gv', t1[i], t_t1[i])
                dump('stt', sttA, t_stt)
        for f4 in range(4):
            fc = cg * 4 + f4
            g = fc // 2
            i = fc % 2
            wl, t_wl = load_lhs(w_in, fc * 128)
            bu = [nb() for _ in range(NH)]
            proj_fm(wl, t_wl, bu)
            for h in range(NH):
                c0 = h * NW
                act(gu[i][:, c0:c0 + NW], PS[bu[h]][:, 0:NW], AF.Gelu, [TP[bu[h]], t_cst], [t_gu[i]], bias=col(C_B + fc))
            for h in range(NH):
                bk = nb()
                for t4 in range(TBH):
                    tb = h * TBH + t4
                    mm(PS[bk][:, t4 * 128:(t4 + 1) * 128], vn[:, tb, f4 * 128:(f4 + 1) * 128], wsb[:, g, :], True, True,
                       [t_vn, t_ws], [TP[bk]])
                c0 = h * NW
                ii = (fc * NH + h) % 2
                stt(t1[ii][:, 0:NW].rearrange("p (a b) -> p a b", a=TBH), PS[bk][:, 0:NW].rearrange("p (a b) -> p a b", a=TBH),
                    col(C_GMG + fc), CT[:, fc:fc + 1, :].to_broadcast([128, TBH, 128]), ALU.mult, ALU.add,
                    [TP[bk], t_CT, t_cst], [t_t1[ii]])
                tt('dve', sgT[:, fc, c0:c0 + NW], t1[ii][:, 0:NW], gu[i][:, c0:c0 + NW], ALU.mult, [t_t1[ii], t_gu[i]], [t_sg])
    if 'sgT' in dbg_out:
        S.barrier()
        for fc in range(16):
            cp('dve', gu[0], sgT[:, fc, :], [t_sg], [t_gu[0]])
            dump('sgT', gu[0], t_gu[0], dbg_out['sgT'][fc * 128:(fc + 1) * 128, :])
    S.barrier()
    for b in [b_vn, b_stA, b_wsb, b_CT] + b_t1:
        AR.release(b)
    gated_branch(w_a, 8192, C_B + 64, sgT, t_sg, add_prev=True)
    if 'mT' in dbg_out:
        S.barrier()
        for fc in range(16):
            cp('dve', xst[0][:, 0:TOK], mT[:, fc, :], [t_m[fc]], [t_xst[0]])
            dump('mT', xst[0][:, 0:TOK], t_xst[0], dbg_out['mT'][fc * 128:(fc + 1) * 128, :])
    S.barrier()
    AR.release(b_sg)
    AR.release(b_xTb)
    for b in b_xst:
        AR.release(b)

    b_x1 = [AR.alloc(2048 * 4) for _ in range(NTB)]
    x1l = [AR.view(b, [128, 2048], F32) for b in b_x1]
    t_x1 = [T() for _ in range(NTB)]
    xv = xtm.rearrange("(n p) d -> p n d", p=128)
    for tb in range(NTB):
        dma(x1l[tb], xv[:, tb, :], writes=[t_x1[tb]])
    b_bc[1] = AR.alloc(2048 * 4)
    bc[1] = AR.view(b_bc[1], [128, 2048], F32)
    dma(bc[0], ln1_g_bc, writes=[t_bc[0]])
    dma(bc[1], ln1_b_bc, writes=[t_bc[1]])
    t_mall = T('mall')
    for cg in range(4):
        load_rhs(w_o, cg * 512, wrh, t_wr)
        for tb in range(NTB):
            bk = nb()
            for k in range(16):
                mm(PS[bk], mT[:, k, tb * 128:(tb + 1) * 128], wrh[:, k, :], k == 0, k == 15, [t_m[k], t_wr], [TP[bk]])
            xs_ = x1l[tb][:, cg * 512:(cg + 1) * 512]
            stt(xs_, xs_, ALPHA, PS[bk], ALU.mult, ALU.add, [TP[bk], t_x1[tb]], [t_x1[tb]])
    S.barrier()
    AR.release(b_m)
    b_x1T = AR.alloc(16 * TOK * 2)
    x1T = AR.view(b_x1T, [128, 16, TOK], BF16)
    t_x1T = T('x1T')
    b_xb = [AR.alloc(2048 * 2) for _ in range(2)]
    xb = [AR.view(b, [128, 2048], BF16) for b in b_xb]
    t_xb = [T(), T()]
    b_st = AR.alloc(64 * 4)
    stt_ = AR.view(b_st, [128, 64], F32)
    t_stt = T()

    def layernorm_tm(xrow, t_x, g_bc, b_bc_, t_g, t_b):
        for q in range(4):
            gen('dve', lambda e, q=q: e.bn_stats(out=stt_[:, q * 6:q * 6 + 6], in_=xrow[:, q * 512:(q + 1) * 512]), [t_x], [t_stt])
        gen('dve', lambda e: e.bn_aggr(out=stt_[:, 32:34], in_=stt_[:, 0:24]), [t_stt], [t_stt])
        tsc('dve', stt_[:, 40:41], stt_[:, 33:34], EPS, None, ALU.add, None, [t_stt], [t_stt])
        act(stt_[:, 40:41], stt_[:, 40:41], AF.Sqrt, [t_stt], [t_stt])
        gen('dve', lambda e: e.reciprocal(out=stt_[:, 40:41], in_=stt_[:, 40:41]), [t_stt], [t_stt])
        tsc('dve', xrow, xrow, stt_[:, 32:33], stt_[:, 40:41], ALU.subtract, ALU.mult, [t_x, t_stt], [t_x])
        tt('dve', xrow, xrow, g_bc, ALU.mult, [t_x, t_g], [t_x])
        tt('dve', xrow, xrow, b_bc_, ALU.add, [t_x, t_b], [t_x])

    def to_featmajor(xrow, t_x, dstT, t_dst, tb):
        i = tb % 2
        cp('act', xb[i], xrow, [t_x], [t_xb[i]])
        for q in range(4):
            bk = nb()
            pbf = PS[bk].bitcast(BF16)
            for k4 in range(4):
                k = q * 4 + k4
                tr(pbf[:, k4 * 128:(k4 + 1) * 128], xb[i][:, k * 128:(k + 1) * 128], ident_b, [t_xb[i], t_id], [TP[bk]])
            cp('act' if q % 2 else 'dve', dstT[:, q * 4:(q + 1) * 4, tb * 128:(tb + 1) * 128],
               pbf[:, 0:512].rearrange("p (a b) -> p a b", a=4), [TP[bk]], [t_dst])

    for tb in range(NTB):
        layernorm_tm(x1l[tb], t_x1[tb], bc[0], bc[1], t_bc[0], t_bc[1])
        to_featmajor(x1l[tb], t_x1[tb], x1T, t_x1T, tb)
        dma(x1_scr[tb * 128:(tb + 1) * 128, :], x1l[tb], reads=[t_x1[tb]])
        dump('x1', x1l[tb], t_x1[tb], dbg_out.get('x1', out)[tb * 128:(tb + 1) * 128, :])
    if stop_after == 'C':
        return finish(nc, S)
    S.barrier()
    for b in b_x1:
        AR.release(b)
    AR.release(b_bc[0]); AR.release(b_bc[1]); AR.release(b_wr)

    b_qT = AR.alloc(16 * TOK * 2)
    qT = AR.view(b_qT, [128, 16, TOK], BF16)
    t_qT = T('qT')
    b_kT = AR.alloc(2048 * 2)
    kTb = AR.view(b_kT, [128, 16, 128], BF16)
    t_kT = T('kT')
    if stop_after == 'D0':
        return finish(nc, S)
    dma(wst[0], keysT, writes=[t_wst[0]])
    if stop_after == 'D1a':
        cp('dve', kTb[:, 0:1, :], wst[0][:, 0:1, :], [t_wst[0]], [t_kT])
        return finish(nc, S)
    if stop_after == 'D1b':
        cp('act', kTb, wst[0], [t_wst[0]], [t_kT])
        return finish(nc, S)
    if stop_after == 'D1c':
        cp('dve', kTb[:, 0:8, :], wst[0][:, 0:8, :], [t_wst[0]], [t_kT])
        cp('dve', kTb[:, 8:16, :], wst[0][:, 8:16, :], [t_wst[0]], [t_kT])
        return finish(nc, S)
    cp('act', kTb, wst[0], [t_wst[0]], [t_kT])
    if stop_after == 'D1':
        return finish(nc, S)
    ld = load_lhs(w_q, 0)
    for qc in range(16):
        nxt = load_lhs(w_q, (qc + 1) * 128) if qc + 1 < 16 else None
        wl, t_wl = ld
        bq = [nb() for _ in range(NH)]
        act_matmul_fm(wl, t_wl, x1T, t_x1T, bq)
        for h in range(NH):
            cp('act' if h % 2 else 'dve', qT[:, qc, h * NW:(h + 1) * NW], PS[bq[h]][:, 0:NW], [TP[bq[h]]], [t_qT])
        ld = nxt

    if stop_after == 'D':
        return finish(nc, S)
    b_s = AR.alloc(2048 * 4)
    sc = AR.view(b_s, [128, 16, 128], F32)
    t_sc = T('sc')
    b_tmp = AR.alloc(256 * 4)
    tmpk = AR.view(b_tmp, [128, 256], F32)
    t_tmpk = T()
    b_T16 = AR.alloc(256 * 4)
    T16 = AR.view(b_T16, [128, 16, 16], F32)
    t_T16 = T()
    b_cand = AR.alloc(2048 * 4)
    cand = AR.view(b_cand, [128, 8, 256], F32)
    t_cand = T()
    b_c24 = AR.alloc(8 * 24 * 4)
    c24 = AR.view(b_c24, [128, 8, 24], F32)
    t_c24 = T()
    b_sm = AR.alloc(64 * 4)
    sm = AR.view(b_sm, [128, 64], F32)
    t_sm = T()
    b_e16 = AR.alloc(128 * 4)
    e16 = AR.view(b_e16, [128, 8, 16], F32)
    t_e16 = T()
    b_tm3 = AR.alloc(4 * 128 * 4)
    tm3 = AR.view(b_tm3, [128, 4, 128], F32)
    t_tm3 = T()
    b_hb3 = AR.alloc(4 * 128 * 4)
    hb3 = AR.view(b_hb3, [128, 4, 128], F32)
    t_hb3 = T()
    NSUB = 8
    b_smat = [AR.alloc(2 * NSUB * 128 * 4) for _ in range(2)]
    smat = [AR.view(b, [128, 2, NSUB, 128], F32) for b in b_smat]
    t_smat = [T(), T()]
    for i_ in range(2):
        gen('dve', lambda e, i_=i_: e.memset(AR.view(b_smat[i_], [128, 2 * NSUB * 128], F32), 0.0), [], [t_smat[i_]])
    b_E = [AR.alloc(128 * 4) for _ in range(2)]
    Et = [AR.view(b, [128, 128], F32) for b in b_E]
    t_E = [T(), T()]
    b_gc = AR.alloc(8 * 128 * 2)
    gcp = AR.view(b_gc, [128, 8, 128], BF16)
    t_gc = [T() for _ in range(8)]
    b_p2 = AR.alloc(8 * 128 * 2)
    p2 = AR.view(b_p2, [128, 8, 128], BF16)
    t_p2 = [T() for _ in range(8)]
    b_gsb = AR.alloc(128 * 128 * 2)
    gsb = AR.view(b_gsb, [128, 128, 128], BF16)
    t_gsb = T()
    gsv = g_scr.rearrange("j i t -> i j t")
    t_sscr = T()
    t_gscr = T()

    import os
    for tb in range(NTB):
        for q4 in range(4):
            bk = nb()
            for k4 in range(4):
                qc = q4 * 4 + k4
                mm(PS[bk][:, k4 * 128:(k4 + 1) * 128], qT[:, qc, tb * 128:(tb + 1) * 128], kTb[:, qc, :], True, True,
                   [t_qT, t_kT], [TP[bk]])
            cp('act' if q4 % 2 else 'dve', sc[:, q4 * 4:(q4 + 1) * 4, :], PS[bk].rearrange("p (a b) -> p a b", a=4), [TP[bk]], [t_sc])
        if 'sc' in dbg_out:
            dump('sc', sc, t_sc, dbg_out['sc'][tb * 128:(tb + 1) * 128, :].rearrange("p (a b) -> p a b", a=16))
        if stop_after == 'R0':
            continue
        for qc in range(16):
            gen('dve', lambda e, qc=qc: e.max(out=T16[:, qc, 0:8], in_=sc[:, qc, :]), [t_sc], [t_T16])
            gen('dve', lambda e, qc=qc: e.match_replace(out=tmpk[:, 0:128], in_to_replace=T16[:, qc, 0:8], in_values=sc[:, qc, :], imm_value=NEG),
                [t_sc, t_T16], [t_tmpk])
            gen('dve', lambda e, qc=qc: e.max(out=T16[:, qc, 8:16], in_=tmpk[:, 0:128]), [t_tmpk], [t_T16])
        T16v = T16.rearrange("p (h c) a -> p h c a", c=2)
        for h in range(8):
            tt('dve', cand[:, h, :].rearrange("p (a b) -> p a b", a=16),
               T16v[:, h, 0, :].unsqueeze(2).to_broadcast([128, 16, 16]),
               T16v[:, h, 1, :].unsqueeze(1).to_broadcast([128, 16, 16]), ALU.add, [t_T16], [t_cand])
        for h in range(8):
            gen('dve', lambda e, h=h: e.max(out=c24[:, h, 0:8], in_=cand[:, h, :]), [t_cand], [t_c24])
            gen('dve', lambda e, h=h: e.match_replace(out=tmpk, in_to_replace=c24[:, h, 0:8], in_values=cand[:, h, :], imm_value=NEG),
                [t_cand, t_c24], [t_tmpk])
            gen('dve', lambda e, h=h: e.max(out=c24[:, h, 8:16], in_=tmpk), [t_tmpk], [t_c24])
            gen('dve', lambda e, h=h: e.match_replace(out=tmpk, in_to_replace=c24[:, h, 8:16], in_values=tmpk, imm_value=NEG),
                [t_tmpk, t_c24], [t_tmpk])
            gen('dve', lambda e, h=h: e.max(out=c24[:, h, 16:24], in_=tmpk), [t_tmpk], [t_c24])
        tt('dve', sm[:, 0:8], c24[:, :, 15], c24[:, :, 16], ALU.add, [t_c24], [t_sm])
        tsc('dve', sm[:, 0:8], sm[:, 0:8], 0.5, None, ALU.mult, None, [t_sm], [t_sm])
        tt('dve', e16, c24[:, :, 0:16], c24[:, :, 0:1].to_broadcast([128, 8, 16]), ALU.subtract, [t_c24], [t_e16])
        act(e16, e16, AF.Exp, [t_e16], [t_e16])
        gen('dve', lambda e: e.tensor_reduce(out=sm[:, 16:24], in_=e16, axis=AX.X, op=ALU.add), [t_e16], [t_sm])
        gen('dve', lambda e: e.reciprocal(out=sm[:, 24:32], in_=sm[:, 16:24]), [t_sm], [t_sm])
        t2 = T16v[:, :, 1, :]
        v3 = tm3.rearrange("p a (h b) -> p a h b", h=8)
        tt('dve', v3[:, 0], t2, c24[:, :, 0:1].to_broadcast([128, 8, 16]), ALU.subtract, [t_T16, t_c24], [t_tm3])
        cp('dve', v3[:, 3], sm[:, 24:32].unsqueeze(2).to_broadcast([128, 8, 16]), [t_sm], [t_tm3])
        tt('dve', v3[:, 1], sm[:, 0:8].unsqueeze(2).to_broadcast([128, 8, 16]), t2, ALU.subtract, [t_T16, t_sm], [t_tm3])
        cp('dve', v3[:, 2], t2, [t_T16], [t_tm3])
        bk = nb()
        for a in range(4):
            mm(PS[bk][:, a * 128:(a + 1) * 128], tm3[:, a, :], ident_f, True, True, [t_tm3, t_id], [TP[bk]])
        cp('dve', hb3, PS[bk].rearrange("p (a b) -> p a b", a=4), [TP[bk]], [t_hb3])
        if 'hb3' in dbg_out:
            dump('hb3', hb3[:, 0:3, :], t_hb3, dbg_out['hb3'][tb * 128:(tb + 1) * 128, :].rearrange("p (a b) -> p a b", a=3))
        if stop_after == 'R1':
            continue
        dma(s_scr[tb * 128:(tb + 1) * 128, :], sc.rearrange("p a b -> p (a b)"), reads=[t_sc], writes=[t_sscr])
        ssv = s_scr.rearrange("t (h c k) -> h c t k", h=8, c=2)
        step = os.environ.get('DBG_STEP', 'z')
        if step == 'a':
            continue
        for sb in range(int(os.environ.get('DBG_NSB', 128 // NSUB))):
            si = sb % 2
            for c in range(2):
                dma(smat[si][0:8, c, :, :], ssv[:, c, tb * 128 + sb * NSUB: tb * 128 + (sb + 1) * NSUB, :],
                    reads=[t_sscr], writes=[t_smat[si]])
            if step == 'b':
                continue
            for t4 in range(int(os.environ.get('DBG_T4', NSUB // 4))):
                tq = sb * NSUB + t4 * 4
                b1, b2, b3 = nb(), nb(), nb()
                mm(PS[b1], rep8, smat[si][:, 0, t4 * 4:(t4 + 1) * 4, :], True, True, [t_id, t_smat[si]], [TP[b1]])
                mm(PS[b2], rep8, smat[si][:, 1, t4 * 4:(t4 + 1) * 4, :], True, True, [t_id, t_smat[si]], [TP[b2]])
                lvl = int(os.environ.get('DBG_LVL', 9))
                for u in range(4 if lvl >= 1 else 0):
                    t = tq + u
                    ei = t % 2
                    s1r = PS[b1][:, u * 128:(u + 1) * 128]
                    s2r = PS[b2][:, u * 128:(u + 1) * 128]
                    tsc('dve', Et[ei], s1r, hb3[:, 0, t:t + 1], None, ALU.add, None, [TP[b1], t_hb3], [t_E[ei]])
                    act(Et[ei], Et[ei], AF.Exp, [t_E[ei]], [t_E[ei]])
                    if lvl < 2:
                        continue
                    stt(gcp[:, t % 8, :], s1r, hb3[:, 1, t:t + 1], Et[ei], ALU.is_ge, ALU.mult, [TP[b1], t_hb3, t_E[ei]], [t_gc[t % 8]])
                    tsc('dve', p2[:, t % 8, :], s2r, hb3[:, 2, t:t + 1], hb3[:, 3, t:t + 1], ALU.is_equal, ALU.mult, [TP[b2], t_hb3], [t_p2[t % 8]])
                    if lvl < 3:
                        continue
                    mm(PS[b3][:, u * 128:(u + 1) * 128], gcp[:, t % 8, :], p2[:, t % 8, :], True, True, [t_gc[t % 8], t_p2[t % 8]], [TP[b3]])
                if lvl >= 4:
                    cp('act', gsb[:, :, tq:tq + 4], PS[b3].rearrange("p (t j) -> p j t", t=4), [TP[b3]], [t_gsb])
        for q in range(8 if not os.environ.get('DBG_NOGDMA') else 0):
            dma(gsv[:, q * 16:(q + 1) * 16, tb * 128:(tb + 1) * 128], gsb[:, q * 16:(q + 1) * 16, :], reads=[t_gsb], writes=[t_gscr])
        if 'G' in dbg_out and tb == 0 and int(os.environ.get('DBG_LVL', 9)) >= 5:
            S.barrier()
            for jq in range(8):
                cp('dve', cand.rearrange("p a b -> p (a b)").rearrange("p (j t) -> p j t", j=16), gsb[:, jq * 16:(jq + 1) * 16, :], [t_gsb], [t_cand])
                dump('G', cand.rearrange("p a b -> p (a b)"), t_cand, dbg_out['G'][:, jq * 2048:(jq + 1) * 2048])
    if stop_after in ('R', 'R1', 'R0'):
        return finish(nc, S)
    S.barrier()
    for b in [b_qT, b_kT, b_s, b_tmp, b_T16, b_cand, b_c24, b_sm, b_e16, b_tm3, b_hb3, b_gc, b_p2, b_gsb] + b_smat + b_E + b_xb:
        AR.release(b)

    b_y = [AR.alloc(2048 * 4) for _ in range(NTB)]
    ysl = [AR.view(b, [128, 2048], F32) for b in b_y]
    t_y = [T() for _ in range(NTB)]
    x1v = x1_scr.rearrange("(n p) d -> p n d", p=128)
    for tb in range(NTB):
        dma(ysl[tb], x1v[:, tb, :], writes=[t_y[tb]])
        tsc('dve', ysl[tb], ysl[tb], ALPHA, None, ALU.mult, None, [t_y[tb]], [t_y[tb]])
    GS = 4
    NV = GS + 1
    b_vst = [AR.alloc(2048 * 4) for _ in range(2)]
    vst = [AR.view(b, [128, 2048], F32) for b in b_vst]
    t_vst = [T(), T()]
    b_vb = [AR.alloc(2048 * 2) for _ in range(NV)]
    vb = [AR.view(b, [128, 2048], BF16) for b in b_vb]
    t_vb = [T() for _ in range(NV)]
    b_at = [AR.alloc(TOK * 2) for _ in range(NV)]
    at = [AR.view(b, [128, TOK], BF16) for b in b_at]
    t_at = [T() for _ in range(NV)]
    b_gch = [AR.alloc(TOK * 2) for _ in range(3)]
    gch = [AR.view(b, [128, TOK], BF16) for b in b_gch]
    t_gch = [T() for _ in range(3)]
    b_hg = [AR.alloc(TOK * 4) for _ in range(2)]
    hg = [AR.view(b, [128, TOK], F32) for b in b_hg]
    t_hg = [T(), T()]
    uTv = uT.rearrange("(k p) j i -> p k j i", p=128)

    def e_load(j):
        i = wcnt[0] % NST
        wcnt[0] += 1
        dma(wst[i], uTv[:, :, j, :], writes=[t_wst[i]])
        jj = wcnt[1] % NLT
        wcnt[1] += 1
        cp('act', wlt[jj], wst[i], [t_wst[i]], [t_wlt[jj]])
        dma(vst[j % 2], vP[j], writes=[t_vst[j % 2]])
        cp('act' if j % 2 else 'dve', vb[j % NV], vst[j % 2], [t_vst[j % 2]], [t_vb[j % NV]])
        dma(gch[j % 3], g_scr[j], reads=[t_gscr], writes=[t_gch[j % 3]])
        return wlt[jj], t_wlt[jj]

    def e_compute(j, ldd):
        ul, t_ul = ldd
        bh = [nb() for _ in range(NH)]
        act_matmul_fm(ul, t_ul, x1T, t_x1T, bh)
        for h in range(NH):
            c0 = h * NW
            act(hg[j % 2][:, c0:c0 + NW], PS[bh[h]][:, 0:NW], AF.Gelu, [TP[bh[h]]], [t_hg[j % 2]])
        tt('dve', at[j % NV], hg[j % 2], gch[j % 3], ALU.mult, [t_hg[j % 2], t_gch[j % 3]], [t_at[j % NV]])

    def e_yphase(j0):
        js = list(range(j0, j0 + GS))
        for tb in range(NTB):
            for cg in range(4):
                bk = nb()
                for n, j in enumerate(js):
                    mm(PS[bk], at[j % NV][:, tb * 128:(tb + 1) * 128], vb[j % NV][:, cg * 512:(cg + 1) * 512], n == 0, n == GS - 1,
                       [t_at[j % NV], t_vb[j % NV]], [TP[bk]])
                ys = ysl[tb][:, cg * 512:(cg + 1) * 512]
                tt('dve', ys, ys, PS[bk], ALU.add, [TP[bk], t_y[tb]], [t_y[tb]])

    ldd = e_load(0)
    for j in range(NCH):
        nxt = e_load(j + 1) if j + 1 < NCH else None
        e_compute(j, ldd)
        if j % GS == GS - 1:
            e_yphase(j - GS + 1)
        ldd = nxt
    S.barrier()
    for b in b_vst + b_vb + b_at + b_gch + b_hg + [b_x1T]:
        AR.release(b)

    for i in range(2):
        b_bc[i] = AR.alloc(2048 * 4)
        bc[i] = AR.view(b_bc[i], [128, 2048], F32)
        b_xb[i] = AR.alloc(2048 * 2)
        xb[i] = AR.view(b_xb[i], [128, 2048], BF16)
    b_wr = AR.alloc(16 * 512 * 2)
    wrh = AR.view(b_wr, [128, 16, 512], BF16)
    dma(bc[0], ln2_g_bc, writes=[t_bc[0]])
    dma(bc[1], ln2_b_bc, writes=[t_bc[1]])
    b_x2T = AR.alloc(16 * TOK * 2)
    x2T = AR.view(b_x2T, [128, 16, TOK], BF16)
    t_x2T = T('x2T')
    for tb in range(NTB):
        layernorm_tm(ysl[tb], t_y[tb], bc[0], bc[1], t_bc[0], t_bc[1])
        to_featmajor(ysl[tb], t_y[tb], x2T, t_x2T, tb)
        if 'x2' in dbg_out:
            dump('x2', ysl[tb], t_y[tb], dbg_out['x2'][tb * 128:(tb + 1) * 128, :])
    b_pT = AR.alloc(2 * TOK * 2)
    pTb = AR.view(b_pT, [128, 2, TOK], BF16)
    t_pT = T()
    b_pst = AR.alloc(1024 * 4)
    pst = AR.view(b_pst, [128, 1024], F32)
    t_pst = T()
    b_wpe = AR.alloc(2 * 2048 * 2)
    wpe = AR.view(b_wpe, [128, 2, 2048], BF16)
    t_wpe = T()
    for kp in range(2):
        dma(pst[:, 0:TOK], pT[kp * 128:(kp + 1) * 128, :], writes=[t_pst])
        cp('dve', pTb[:, kp, :], pst[:, 0:TOK], [t_pst], [t_pT])
    for kp in range(2):
        for hh in range(2):
            dma(pst, w_pe[kp * 128:(kp + 1) * 128, hh * 1024:(hh + 1) * 1024], writes=[t_pst])
            cp('dve', wpe[:, kp, hh * 1024:(hh + 1) * 1024], pst, [t_pst], [t_wpe])
    b_o = [AR.alloc(512 * 4) for _ in range(2)]
    ot = [AR.view(b, [128, 512], F32) for b in b_o]
    t_ot = [T(), T()]
    it_ = 0
    for cg in range(4):
        load_rhs(w_pg, cg * 512, wrh, t_wr)
        for tb in range(NTB):
            i = it_ % 2
            it_ += 1
            bz, bp = nb(), nb()
            for k in range(16):
                mm(PS[bz], x2T[:, k, tb * 128:(tb + 1) * 128], wrh[:, k, :], k == 0, k == 15, [t_x2T, t_wr], [TP[bz]])
            for kp in range(2):
                mm(PS[bp], pTb[:, kp, tb * 128:(tb + 1) * 128], wpe[:, kp, cg * 512:(cg + 1) * 512], kp == 0, kp == 1, [t_pT, t_wpe], [TP[bp]])
            act(ot[i], PS[bz], AF.Sigmoid, [TP[bz]], [t_ot[i]])
            tt('dve', ot[i], ot[i], PS[bp], ALU.mult, [TP[bp], t_ot[i]], [t_ot[i]])
            tt('dve', ot[i], ot[i], ysl[tb][:, cg * 512:(cg + 1) * 512], ALU.add, [t_ot[i], t_y[tb]], [t_ot[i]])
            dma(out[tb * 128:(tb + 1) * 128, cg * 512:(cg + 1) * 512], ot[i], reads=[t_ot[i]])
    return finish(nc, S)


def finish(nc, S):
    sems = {e: nc.alloc_semaphore(name=f"s_{e}") for e in CENGS}
    dsems = [nc.alloc_semaphore(name=f"d_{i}") for i in range(KDMA)]
    S.emit(sems, dsems)
    return nc


def prep_shared(inp):
    f = np.float32
    g = lambda k: np.asarray(inp[k], dtype=f)[0]
    b_in = g('b_in')
    cst = np.zeros((128, CST_COLS), f)
    cst[:, C_B:C_B + 96] = b_in.reshape(96, 128).T
    cst[:, C_GMG:C_GMG + 16] = g('gm_ln_g').reshape(16, 128).T
    cst[:, C_CVB:C_CVB + 16] = g('cv_b').reshape(16, 128).T
    cst[:, C_CVG:C_CVG + 16] = g('cv_ln_g').reshape(16, 128).T
    cst[:, C_CVBB:C_CVBB + 16] = g('cv_ln_b').reshape(16, 128).T
    cvw = g('cv_w')
    cst[:, C_CVW:C_CVW + 496] = cvw.reshape(31, 16, 128).transpose(2, 1, 0).reshape(128, 496)
    rep = lambda v: np.ascontiguousarray(np.broadcast_to(v[None, :], (128, v.shape[0])))
    sh = dict(
        w_in=g('w_in'), b_v_bc=rep(b_in[2048:4096]), gm_b_bc=rep(g('gm_ln_b')),
        gm_wsT=np.ascontiguousarray(g('gm_ws').transpose(2, 0, 1)),
        gm_bs=np.ascontiguousarray(g('gm_bs').reshape(1, 1024)),
        w_a=g('w_gm_out'), w_b=g('w_cv_out'), w_o=g('w_o'),
        ln1_g_bc=rep(g('ln1_g')), ln1_b_bc=rep(g('ln1_b')),
        w_q=g('peer_wq'),
        keysT=np.ascontiguousarray(g('peer_keys').reshape(16, 128, 128).transpose(2, 0, 1)),
        uT=np.ascontiguousarray(g('peer_u').reshape(128, 128, 2048).transpose(2, 1, 0)),
        vP=np.ascontiguousarray(g('peer_v').reshape(128, 128, 2048).transpose(1, 0, 2)),
        ln2_g_bc=rep(g('ln2_g')), ln2_b_bc=rep(g('ln2_b')),
        w_pg=g('ple_w_gate'), w_pe=g('ple_w_proj'),
    )
    return sh, cst


def prep_core(inp, cst, b, t0, TOK):
    f = np.float32
    x = np.asarray(inp['x'], dtype=f)
    p = np.asarray(inp['p'], dtype=f)[0]
    xs = x[b, t0:t0 + TOK]
    xT = np.zeros((2048, TOK + 32), f)
    xT[:, 32:] = xs.T
    c = cst.copy()
    if t0 >= 32:
        xT[:, :32] = x[b, t0 - 32:t0].T
        c[:, C_FLAG] = 1.0
    else:
        c[:, C_FLAG] = 0.0
    return dict(xT=xT, x=np.ascontiguousarray(xs), pT=np.ascontiguousarray(p[b, t0:t0 + TOK].T), cst=c)


_NC_CACHE = {}


def kernel(**inputs):
    TOK = 1024
    sh, cst = prep_shared(inputs)
    if 'nc' not in _NC_CACHE:
        _NC_CACHE['nc'] = build(NTB=8)
    nc = _NC_CACHE['nc']
    in_maps = []
    for c in range(8):
        b, half = c // 2, c % 2
        m = dict(sh)
        m.update(prep_core(inputs, cst, b, half * TOK, TOK))
        in_maps.append(m)
    res = run_bass_kernel_spmd(nc, in_maps, core_ids=list(range(8)))
    outp = np.zeros((4, 2048, 2048), np.float32)
    for c in range(8):
        b, half = c // 2, c % 2
        outp[b, half * TOK:(half + 1) * TOK] = res.results[c]["out"]
    return outp
```
